# Optimizing a Trainium2 kernel written in Bass

```python
import math
import jax, jax.numpy as jnp
from jax import lax
import numpy as np

D_MODEL = 1024
BATCH = 4
SEQ = 4096
DEPTH = 2

N_BRANCH = 4
MIX_W = D_MODEL // N_BRANCH
HEAD_DIM = 64
BLOCK = 128
A_HEADS = MIX_W // HEAD_DIM
A_KV = A_HEADS // 2
WINDOW = 128
B_HEADS = MIX_W // HEAD_DIM
B_KV = B_HEADS // 2
ROPE_THETA = 10000.0
GRID_W = 64
C_VDIM = HEAD_DIM
C_DIM = C_VDIM // 2
C_HEADS = MIX_W // C_VDIM
C_KV = C_HEADS // 2
D_HEADS = 4
D_KDIM = MIX_W // D_HEADS
D_VDIM = MIX_W // D_HEADS
CHUNK = 64
N_META = 16
FRONT = (-N_META) % BLOCK
META_END = FRONT + N_META
N_BUCKETS = 32
MAX_DIST = 128
D_FF = -(-(8 * D_MODEL) // (3 * 256)) * 256
RMS_EPS = 1e-6
MASK_VALUE = -1e30
F_FLOOR = 1e-30
IN_WIDTHS = (A_HEADS * HEAD_DIM, A_KV * HEAD_DIM, A_KV * HEAD_DIM,
             B_HEADS * HEAD_DIM, B_KV * HEAD_DIM, B_KV * HEAD_DIM,
             C_HEADS * 2 * C_DIM, C_KV * 2 * C_DIM, C_KV * C_VDIM,
             D_HEADS * D_KDIM, D_HEADS * D_KDIM, D_HEADS * D_KDIM, D_HEADS * D_VDIM, D_HEADS * D_VDIM,
             N_BRANCH * D_MODEL)
IN_TOTAL = sum(IN_WIDTHS)

kernel_name = 'hybrid_gated_parallel_encoder'


def rms_norm(x, g):
    x32 = x.astype(jnp.float32)
    y = x32 * lax.rsqrt(jnp.mean(x32 * x32, axis=-1, keepdims=True) + RMS_EPS)
    return (y * g.astype(jnp.float32)).astype(x.dtype)


def split_in(proj):
    idx, acc = [], 0
    for w in IN_WIDTHS[:-1]:
        acc += w
        idx.append(acc)
    return jnp.split(proj, idx, axis=-1)


def t5_bucket(rel):
    half = N_BUCKETS // 2
    exact = half // 2
    n = jnp.abs(rel)
    nf = jnp.maximum(n, exact).astype(jnp.float32)
    big = exact + (jnp.log(nf / exact) / math.log(MAX_DIST / exact) * (half - exact)).astype(jnp.int32)
    big = jnp.minimum(big, half - 1)
    return jnp.where(rel > 0, half, 0) + jnp.where(n < exact, n, big)


def windowed_sink_attention(q, k, v, sink, bias_tab):
    Bn, L = q.shape[:2]
    nb = L // BLOCK
    G = A_HEADS // A_KV
    qb = q.reshape(Bn, nb, BLOCK, A_KV, G, HEAD_DIM)

    def band(t):
        tp = jnp.pad(t, ((0, 0), (BLOCK, BLOCK), (0, 0), (0, 0)))
        tb = tp.reshape(Bn, nb + 2, BLOCK, A_KV, HEAD_DIM)
        return jnp.concatenate([tb[:, :-2], tb[:, 1:-1], tb[:, 2:]], axis=2)

    kb, vb = band(k), band(v)
    km, vm = k[:, FRONT:META_END], v[:, FRONT:META_END]
    qpos = jnp.arange(L).reshape(nb, BLOCK)
    kpos = qpos[:, :1] - BLOCK + jnp.arange(3 * BLOCK)[None, :]
    rel_b = kpos[:, None, :] - qpos[:, :, None]
    ok_b = (jnp.abs(rel_b) <= WINDOW) & (kpos >= META_END)[:, None, :] & (kpos < L)[:, None, :]
    rel_m = jnp.arange(FRONT, META_END)[None, None, :] - qpos[:, :, None]

    def head_bias(rel):
        b = bias_tab[t5_bucket(rel)]
        return jnp.moveaxis(b, -1, 0).reshape((A_KV, G) + rel.shape).astype(jnp.float32)

    scale = HEAD_DIM ** -0.5
    s_b = jnp.einsum('bnqhgd,bnkhd->bhgnqk', qb, kb).astype(jnp.float32) * scale + head_bias(rel_b)
    s_b = jnp.where(ok_b, s_b, MASK_VALUE)
    s_m = jnp.einsum('bnqhgd,bmhd->bhgnqm', qb, km).astype(jnp.float32) * scale + head_bias(rel_m)
    s_sink = jnp.broadcast_to(sink.astype(jnp.float32).reshape(A_KV, G, 1, 1, 1), s_m.shape[:-1] + (1,))
    p = jax.nn.softmax(jnp.concatenate([s_b, s_m, s_sink], axis=-1), axis=-1).astype(v.dtype)
    p_b = p[..., :3 * BLOCK]
    p_m = p[..., 3 * BLOCK:3 * BLOCK + N_META]
    o = jnp.einsum('bhgnqk,bnkhd->bnqhgd', p_b, vb) + jnp.einsum('bhgnqm,bmhd->bnqhgd', p_m, vm)
    return o.reshape(Bn, L, A_HEADS * HEAD_DIM)


def rotate_half_pairs(x, ang):
    c = jnp.cos(ang)[:, None, :]
    s = jnp.sin(ang)[:, None, :]
    x1, x2 = jnp.split(x, 2, axis=-1)
    return jnp.concatenate([x1 * c - x2 * s, x2 * c + x1 * s], axis=-1)


def axial_rope(x, row, col):
    half = x.shape[-1] // 2
    inv = ROPE_THETA ** (-jnp.arange(0, half, 2, dtype=jnp.float32) / half)
    x32 = x.astype(jnp.float32)
    xr = rotate_half_pairs(x32[..., :half], row[:, None] * inv[None, :])
    xc = rotate_half_pairs(x32[..., half:], col[:, None] * inv[None, :])
    return jnp.concatenate([xr, xc], axis=-1).astype(x.dtype)


def axial_rope_attention(q, k, v, row, col, key_ok):
    Bn, L = q.shape[:2]
    nb = L // BLOCK
    G = B_HEADS // B_KV
    q = axial_rope(q, row, col)
    k = axial_rope(k, row, col)
    qb = jnp.moveaxis(q.reshape(Bn, nb, BLOCK, B_KV, G, HEAD_DIM), 1, 0)
    scale = HEAD_DIM ** -0.5

    def one_block(qblk):
        s = jnp.einsum('bqhgd,bkhd->bhgqk', qblk, k).astype(jnp.float32) * scale
        s = jnp.where(key_ok, s, MASK_VALUE)
        p = jax.nn.softmax(s, axis=-1).astype(v.dtype)
        return jnp.einsum('bhgqk,bkhd->bqhgd', p, v)

    o = lax.map(one_block, qb)
    return jnp.moveaxis(o, 0, 1).reshape(Bn, L, B_HEADS * HEAD_DIM)


def differential_attention(q, k, v, lam, bias_tab, key_ok, sub_g, lambda_init):
    Bn, L = q.shape[:2]
    nb = L // BLOCK
    G = C_HEADS // C_KV
    qb = jnp.moveaxis(q.reshape(Bn, nb, BLOCK, C_KV, G, 2, C_DIM), 1, 0)
    qpos = jnp.arange(L).reshape(nb, BLOCK)
    kpos = jnp.arange(L)
    scale = C_DIM ** -0.5

    def one_block(args):
        qblk, qp = args
        bias = bias_tab[t5_bucket(kpos[None, :] - qp[:, None])]
        bias = jnp.moveaxis(bias, -1, 0).reshape(C_KV, G, BLOCK, L).astype(jnp.float32)
        s = jnp.einsum('bqhgcd,bkhcd->bchgqk', qblk, k).astype(jnp.float32) * scale + bias
        s = jnp.where(key_ok, s, MASK_VALUE)
        p = jax.nn.softmax(s, axis=-1)
        a = p[:, 0] - lam * p[:, 1]
        return jnp.einsum('bhgqk,bkhd->bqhgd', a.astype(v.dtype), v)

    o = lax.map(one_block, (qb, qpos))
    o = jnp.moveaxis(o, 0, 1).reshape(Bn, L, C_HEADS, C_VDIM)
    o = rms_norm(o, sub_g) * (1.0 - lambda_init)
    return o.reshape(Bn, L, C_HEADS * C_VDIM)


def chunk_gla(q, k, v, log_f):
    Bn, L, H, dk = q.shape
    dv = v.shape[-1]
    n = L // CHUNK

    def to_chunks(t):
        return t.reshape(Bn, n, CHUNK, H, t.shape[-1]).transpose(1, 0, 3, 2, 4)

    tri = jnp.tril(jnp.ones((CHUNK, CHUNK), bool))[:, :, None]

    def step(S, inp):
        qi, ki, vi, gi = inp
        b = jnp.cumsum(gi, axis=2)
        diff = b[:, :, :, None, :] - b[:, :, None, :, :]
        decay = jnp.where(tri, jnp.exp(jnp.where(tri, diff, 0.0)), 0.0)
        attn = jnp.einsum('bhtk,bhtsk,bhsk->bhts', qi, decay, ki)
        o = jnp.einsum('bhts,bhsv->bhtv', attn, vi) + jnp.einsum('bhtk,bhkv->bhtv', qi * jnp.exp(b), S)
        b_last = b[:, :, -1:, :]
        S = S * jnp.exp(b_last[:, :, 0, :, None]) + jnp.einsum('bhsk,bhsv->bhkv', ki * jnp.exp(b_last - b), vi)
        return S, o

    S0 = jnp.zeros((Bn, H, dk, dv), jnp.float32)
    _, o = lax.scan(step, S0, (to_chunks(q), to_chunks(k), to_chunks(v), to_chunks(log_f)))
    return o.transpose(1, 0, 3, 2, 4).reshape(Bn, L, H, dv)


def hgrn2_bidirectional(q, zf, zb, i, g, lb_f, lb_b, valid, out_g):
    Bn, L = q.shape[:2]

    def heads(t, dh):
        return t.reshape(Bn, L, D_HEADS, dh).astype(jnp.float32)

    qh = heads(q, D_KDIM) * (D_KDIM ** -0.5)
    vh = heads(i, D_VDIM)
    vmask = valid[None, :, None, None]

    def gates(z, lb):
        z = heads(z, D_KDIM)
        lb = lb.astype(jnp.float32).reshape(D_HEADS, D_KDIM)
        f = lb + (1.0 - lb) * jax.nn.sigmoid(z)
        log_f = jnp.log(jnp.maximum(f, F_FLOOR))
        kk = (1.0 - lb) * jax.nn.sigmoid(-z) * vmask
        return log_f, kk

    lf_f, k_f = gates(zf, lb_f)
    lf_b, k_b = gates(zb, lb_b)
    flip = lambda t: jnp.flip(t, axis=1)
    o_f = chunk_gla(qh, k_f, vh, lf_f)
    o_b = flip(chunk_gla(flip(qh), flip(k_b), flip(vh), flip(lf_b)))
    o = rms_norm(o_f + o_b, out_g) * jax.nn.silu(heads(g, D_VDIM))
    return o.reshape(Bn, L, D_HEADS * D_VDIM).astype(q.dtype)


def setup_inputs(seed: int = 0) -> dict:
    key = jax.random.key(seed)
    ks = jax.random.split(key, 20)
    nrm = jax.random.normal
    f32 = jnp.float32
    return {
        'x': nrm(ks[0], (BATCH, SEQ, D_MODEL), f32),
        'meta_tokens': nrm(ks[1], (N_META, D_MODEL), f32),
        'rel_bias': 0.5 * nrm(ks[2], (N_BUCKETS, A_HEADS + C_HEADS), f32),
        'hgrn_lb_logits': 0.5 * nrm(ks[3], (2, DEPTH, D_HEADS * D_KDIM), f32),
        'ln_mix': 1.0 + 0.05 * nrm(ks[4], (DEPTH, D_MODEL), f32),
        'w_in': nrm(ks[5], (DEPTH, D_MODEL, IN_TOTAL), f32) * D_MODEL ** -0.5,
        'attn_sink': 0.5 * nrm(ks[6], (DEPTH, A_HEADS), f32),
        'qk_norm_q': 1.0 + 0.05 * nrm(ks[7], (DEPTH, HEAD_DIM), f32),
        'qk_norm_k': 1.0 + 0.05 * nrm(ks[8], (DEPTH, HEAD_DIM), f32),
        'diff_lambda': 0.1 * nrm(ks[9], (DEPTH, 4, C_DIM), f32),
        'diff_subnorm': 1.0 + 0.05 * nrm(ks[10], (DEPTH, C_VDIM), f32),
        'hgrn_out_norm': 1.0 + 0.05 * nrm(ks[11], (DEPTH, D_VDIM), f32),
        'w_branch': nrm(ks[12], (DEPTH, N_BRANCH, MIX_W, D_MODEL), f32) * MIX_W ** -0.5,
        'w_out': nrm(ks[13], (DEPTH, D_MODEL, D_MODEL), f32) * D_MODEL ** -0.5,
        'ln_ffn': 1.0 + 0.05 * nrm(ks[14], (DEPTH, D_MODEL), f32),
        'w_ffn_gate': nrm(ks[15], (DEPTH, D_MODEL, D_FF), f32) * D_MODEL ** -0.5,
        'w_ffn_up': nrm(ks[16], (DEPTH, D_MODEL, D_FF), f32) * D_MODEL ** -0.5,
        'w_ffn_down': nrm(ks[17], (DEPTH, D_FF, D_MODEL), f32) * D_FF ** -0.5,
        'ln_final': 1.0 + 0.05 * nrm(ks[18], (D_MODEL,), f32),
    }


def reference(x, meta_tokens, rel_bias, hgrn_lb_logits, ln_mix, w_in, attn_sink, qk_norm_q, qk_norm_k,
              diff_lambda, diff_subnorm, hgrn_out_norm, w_branch, w_out, ln_ffn, w_ffn_gate, w_ffn_up,
              w_ffn_down, ln_final):
    Bn, S, _ = x.shape
    ROWS = S // GRID_W
    L = META_END + S
    h = jnp.concatenate([jnp.zeros((Bn, FRONT, D_MODEL), x.dtype),
                         jnp.broadcast_to(meta_tokens.astype(x.dtype)[None], (Bn, N_META, D_MODEL)),
                         x], axis=1)
    pos = jnp.arange(L)
    key_ok = pos >= FRONT
    valid = key_ok.astype(jnp.float32)
    row = jnp.concatenate([jnp.zeros((FRONT,), jnp.int32), -jnp.ones((N_META,), jnp.int32),
                           jnp.repeat(jnp.arange(ROWS, dtype=jnp.int32), GRID_W)]).astype(jnp.float32)
    col = jnp.concatenate([jnp.zeros((FRONT,), jnp.int32), jnp.arange(N_META, dtype=jnp.int32),
                           jnp.tile(jnp.arange(GRID_W, dtype=jnp.int32), ROWS)]).astype(jnp.float32)
    lb_p = jax.nn.softmax(hgrn_lb_logits.astype(jnp.float32), axis=1)
    lb_all = jnp.cumsum(lb_p, axis=1) - lb_p[:, :1]
    bias_a = rel_bias[:, :A_HEADS]
    bias_c = rel_bias[:, A_HEADS:]

    for l in range(DEPTH):
        u = rms_norm(h, ln_mix[l])
        (aq, ak, av, bq, bk, bv, cq, ck, cv, dq, dzf, dzb, di, dg, gz) = split_in(u @ w_in[l])
        y_a = windowed_sink_attention(aq.reshape(Bn, L, A_HEADS, HEAD_DIM), ak.reshape(Bn, L, A_KV, HEAD_DIM),
                                      av.reshape(Bn, L, A_KV, HEAD_DIM), attn_sink[l], bias_a)
        y_b = axial_rope_attention(rms_norm(bq.reshape(Bn, L, B_HEADS, HEAD_DIM), qk_norm_q[l]),
                                   rms_norm(bk.reshape(Bn, L, B_KV, HEAD_DIM), qk_norm_k[l]),
                                   bv.reshape(Bn, L, B_KV, HEAD_DIM), row, col, key_ok)
        lam_init = 0.8 - 0.6 * math.exp(-0.3 * l)
        lam_p = diff_lambda[l].astype(jnp.float32)
        lam = jnp.exp(jnp.sum(lam_p[0] * lam_p[1])) - jnp.exp(jnp.sum(lam_p[2] * lam_p[3])) + lam_init
        y_c = differential_attention(cq.reshape(Bn, L, C_HEADS, 2, C_DIM), ck.reshape(Bn, L, C_KV, 2, C_DIM),
                                     cv.reshape(Bn, L, C_KV, C_VDIM), lam, bias_c, key_ok, diff_subnorm[l], lam_init)
        y_d = hgrn2_bidirectional(dq, dzf, dzb, di, dg, lb_all[0, l], lb_all[1, l], valid, hgrn_out_norm[l])
        gate = jax.nn.sigmoid(gz.astype(jnp.float32)).astype(h.dtype).reshape(Bn, L, N_BRANCH, D_MODEL)
        merged = gate[:, :, 0] * (y_a @ w_branch[l, 0])
        for n, y in enumerate((y_b, y_c, y_d), start=1):
            merged = merged + gate[:, :, n] * (y @ w_branch[l, n])
        h = h + merged @ w_out[l]
        u = rms_norm(h, ln_ffn[l])
        h = h + (jax.nn.silu(u @ w_ffn_gate[l]) * (u @ w_ffn_up[l])) @ w_ffn_down[l]

    return rms_norm(h, ln_final)[:, META_END:]
```

```python
from contextlib import ExitStack
import math
import numpy as np
import concourse.bass as bass
import concourse.mybir as mybir
from concourse.bass_utils import run_bass_kernel_spmd

F32 = mybir.dt.float32
BF16 = mybir.dt.bfloat16
AF = mybir.ActivationFunctionType
ALU = mybir.AluOpType

D_MODEL = 1024
SEQ = 4096
N_META = 16
FRONT = 112
L = 4224
NT = 33
TPG = 3
G = 384
NG = 11
D_FF = 2816
NFF = 22
EPS = 1e-6
NF = 18
NTM = 640
NMIX = NF * 128 + NTM
CH = 64
NCH = L // CH
BIG = 30000.0
SC_A = 0.125
SC_C = 32 ** -0.5


class Buf:
    __slots__ = ("name", "w", "r")

    def __init__(self, name=""):
        self.name = name
        self.w = []
        self.r = []


class T:
    def __init__(self, h, name):
        self.h = h
        self.b = Buf(name)

    def __getitem__(self, k):
        return self.h[k]


class FW:
    NSLOT = 8

    def __init__(self, nc):
        self.nc = nc
        self.engs = {"pe": nc.tensor, "dve": nc.vector, "act": nc.scalar,
                     "pool": nc.gpsimd, "sp": nc.sync}
        self.sem = {}
        self.cnt = {}
        for e in ("pe", "dve", "act", "pool"):
            self.sem[e] = nc.alloc_semaphore("s_" + e)
            self.cnt[e] = 0
        self.slots = {}
        self.slot_rr = {}
        for q in ("sp", "pool"):
            names = []
            for i in range(self.NSLOT):
                n = "d_%s%d" % (q, i)
                self.sem[n] = nc.alloc_semaphore(n)
                self.cnt[n] = 0
                names.append(n)
            self.slots[q] = names
            self.slot_rr[q] = 0
        self.known = {e: {} for e in self.engs}
        self.ninst = 0

    def _wait(self, e, dep):
        s, v = dep
        if self.known[e].get(s, 0) >= v:
            return
        self.engs[e].wait_ge(self.sem[s], v)
        self.known[e][s] = v
        self.ninst += 1

    def _deps(self, e, reads, writes):
        for b in reads:
            for d in b.w:
                self._wait(e, d)
        for b in writes:
            for d in b.w:
                if d[0] != e:
                    self._wait(e, d)
            for d in b.r:
                if d[0] != e:
                    self._wait(e, d)

    def _mark(self, tag, reads, writes, add):
        for b in reads:
            b.r.append(tag)
            if len(b.r) > 24:
                b.r = b.r[-24:] if False else _compress(b.r)
        for b in writes:
            if add:
                b.w.append(tag)
            else:
                b.w = [tag]
                b.r = []

    def op(self, e, fn, reads=(), writes=(), add=False):
        self._deps(e, reads, writes)
        ins = fn(self.engs[e])
        ins.then_inc(self.sem[e], 1)
        self.cnt[e] += 1
        self._mark((e, self.cnt[e]), reads, writes, add)
        self.ninst += 1
        return ins

    def dma(self, q, out, in_, reads=(), writes=(), add=False, **kw):
        names = self.slots[q]
        s = names[self.slot_rr[q] % len(names)]
        self.slot_rr[q] += 1
        if self.cnt[s] > 0:
            self._wait(q, (s, self.cnt[s]))
        self._deps(q, reads, writes)
        ins = self.engs[q].dma_start(out=out, in_=in_, **kw)
        self.cnt[s] += 16
        ins.then_inc(self.sem[s], 16)
        self._mark((s, self.cnt[s]), reads, writes, add)
        self.ninst += 1
        return ins

    def barrier(self):
        for e in self.engs:
            for s, v in self.cnt.items():
                if v > 0:
                    self._wait(e, (s, v))


def _compress(tags):
    best = {}
    for s, v in tags:
        if best.get(s, 0) < v:
            best[s] = v
    return list(best.items())


class Ctx:
    pass


_UID = [0]


def mktile(C, st, name, shape, dtype):
    _UID[0] += 1
    name = "%s_%d" % (name, _UID[0])
    h = st.enter_context(C.nc.sbuf_tensor(name, shape, dtype))
    return T(h, name)


def mkpsum(C, st, name, shape, dtype=F32):
    _UID[0] += 1
    name = "%s_%d" % (name, _UID[0])
    h = st.enter_context(C.nc.psum_tensor(name, shape, dtype))
    return T(h, name)


def t5_bucket_np(rel):
    n = np.abs(rel)
    nf = np.maximum(n, 8).astype(np.float32)
    big = 8 + (np.log(nf / np.float32(8)) / np.float32(math.log(128 / 8)) * np.float32(8)).astype(np.int32)
    big = np.minimum(big, 15)
    return np.where(rel > 0, 16, 0) + np.where(n < 8, n, big)


def host_constants():
    c = {}
    pos = np.arange(L)
    row = np.zeros(L, np.float32)
    col = np.zeros(L, np.float32)
    row[FRONT:FRONT + N_META] = -1.0
    col[FRONT:FRONT + N_META] = np.arange(N_META)
    tok = np.arange(SEQ)
    row[128:] = (tok // 64).astype(np.float32)
    col[128:] = (tok % 64).astype(np.float32)
    half = 32
    inv = (np.float32(10000.0) ** (-np.arange(0, half, 2, dtype=np.float32) / np.float32(half))).astype(np.float32)
    cs = np.zeros((2, 128, L), np.float32)
    for d in range(128):
        dd = d % 64
        p = row if dd < 32 else col
        j = (dd % 32) % 16
        ang = (p * inv[j]).astype(np.float32)
        cs[0, d] = np.cos(ang)
        cs[1, d] = np.sin(ang)
    c["cs_tab"] = cs
    cm = np.zeros((128, 5, 128), np.float32)
    cm[:, 0, :] = np.eye(128)
    cm[:, 1, :] = np.eye(128)[::-1]
    R = np.zeros((128, 128), np.float32)
    for d in range(128):
        e = d % 32
        if e < 16:
            R[d + 16, d] = -1.0
        else:
            R[d - 16, d] = 1.0
    cm[:, 2, :] = R
    cm[0:64, 3, 0:64] = 1.0
    cm[64:128, 3, 64:128] = 1.0
    cm[:, 4, :] = 1.0
    c["cmat"] = cm
    oh = np.zeros((33, 512), np.float32)
    rel = np.arange(512) - 256
    b = t5_bucket_np(rel)
    oh[b, np.arange(512)] = 1.0
    oh[32, :] = (np.abs(rel) > 128).astype(np.float32)
    c["oh_tab"] = oh
    hm = np.zeros((64, 2, 64), np.float32)
    s = np.arange(64)[:, None]
    t = np.arange(64)[None, :]
    hm[:, 0, :] = (s <= t)
    hm[:, 1, :] = (s >= t)
    c["hmask"] = hm
    rs = np.ones((64, L), np.float32)
    rs[:, ::CH] = 0.0
    c["scan_rs"] = rs
    return c


def feature_cols():
    cols = []

    def heads(base, hs):
        out = []
        for h in hs:
            out += list(range(base + h * 64, base + h * 64 + 64))
        return out
    cols += heads(0, (0, 2)) + heads(0, (1, 3)) + list(range(256, 384))
    cols += heads(512, (0, 2)) + heads(512, (1, 3)) + list(range(768, 896))
    cols += heads(1024, (0, 2)) + heads(1024, (1, 3))
    z32 = [-1] * 32
    ck = 1280
    cols += list(range(ck, ck + 32)) + z32 + list(range(ck + 64, ck + 96)) + z32
    cols += z32 + list(range(ck + 32, ck + 64)) + z32 + list(range(ck + 96, ck + 128))
    cols += list(range(1536, 1792))
    cols += list(range(1792, 2048))
    cols += list(range(2048, 2304))
    cols += list(range(2560, 2816))
    assert len(cols) == NF * 128
    cols += list(range(384, 512)) + list(range(896, 1024)) + list(range(1408, 1536)) + list(range(2304, 2560))
    assert len(cols) == NMIX
    return np.array(cols)


def build(nlayers=2, debug=False, stop_after=None):
    nc = bass.Bass("TRN2", target_bir_lowering=False)
    C = Ctx()
    C.nc = nc
    C.fw = fw = FW(nc)
    C.debug = debug

    def din(name, shape):
        return nc.dram_tensor(name, shape, F32, kind="ExternalInput").ap()

    C.x = din("x", [SEQ, D_MODEL])
    C.meta = din("meta_tokens", [N_META, D_MODEL])
    C.rel_bias = din("rel_bias", [32, 8])
    C.lb_logits = din("hgrn_lb_logits", [2, 2, 256])
    C.ln_mix = din("ln_mix", [2, D_MODEL])
    C.w_mix = din("w_mix", [2, D_MODEL, NMIX])
    C.w_gz = din("w_gz", [2, D_MODEL, 4096])
    C.attn_sink = din("attn_sink", [2, 4])
    C.qk_q = din("qk_norm_q", [2, 64])
    C.qk_k = din("qk_norm_k", [2, 64])
    C.diff_lambda = din("diff_lambda", [2, 128])
    C.diff_subnorm = din("diff_subnorm", [2, 64])
    C.hgrn_out_norm = din("hgrn_out_norm", [2, 64])
    C.w_branch = din("w_branch", [2, 4, 256, D_MODEL])
    C.w_out = din("w_out", [2, D_MODEL, D_MODEL])
    C.ln_ffn = din("ln_ffn", [2, D_MODEL])
    C.w_g = din("w_ffn_gate", [2, D_MODEL, D_FF])
    C.w_u = din("w_ffn_up", [2, D_MODEL, D_FF])
    C.w_d = din("w_ffn_down", [2, D_FF, D_MODEL])
    C.ln_final = din("ln_final", [1, D_MODEL])
    C.cs_tab = din("cs_tab", [2, 128, L])
    C.cmat = din("cmat", [128, 5, 128])
    C.oh_tab = din("oh_tab", [33, 512])
    C.hmask = din("hmask", [64, 2, 64])
    C.scan_rs = din("scan_rs", [64, L])
    C.out = nc.dram_tensor("out", [SEQ, D_MODEL], F32, kind="ExternalOutput").ap()

    dk = "ExternalOutput" if debug else "Internal"

    def dscr(name, shape, dt):
        return nc.dram_tensor(name, shape, dt, kind=dk).ap()

    C.Hd = dscr("Hd", [L, D_MODEL], F32)
    C.UTd = dscr("UTd", [8, 128, L], BF16)
    C.Fd = dscr("Fd", [NF, 128, L], BF16)
    C.LFd = dscr("LFd", [4, 128, L], F32)
    C.Td = dscr("Td", [L, NTM], BF16)
    C.Yd = dscr("Yd", [16, 64, L], BF16)
    C.fvec = dscr("fvec", [16, 512], F32)

    with ExitStack() as top:
        C.ident = mktile(C, top, "ident", [128, 128], BF16)
        C.Jf = mktile(C, top, "Jf", [128, 128], F32)
        C.Rm = mktile(C, top, "Rm", [128, 128], BF16)
        C.blk64 = mktile(C, top, "blk64", [128, 128], BF16)
        C.E65 = mktile(C, top, "E65", [128, 64], F32)
        C.TbT = mktile(C, top, "TbT", [128, 36, 128], BF16)
        C.cfar = mktile(C, top, "cfar", [128, 16], F32)
        C.small = mktile(C, top, "small", [128, 64], F32)
        C.lbt = mktile(C, top, "lbt", [128, 4, 2], F32)
        phase_consts(C)
        for l in range(nlayers):
            layer_params(C, l)
            phase_p1(C, l)
            if stop_after == ("p1", l):
                break
            phase_attn(C, l)
            if stop_after == ("attn", l):
                break
            phase_hgrn(C, l)
            if stop_after == ("p2", l):
                break
            phase_p3a(C, l)
            if stop_after == ("p3a", l):
                break
            phase_p3b(C, l, last=(l == nlayers - 1))
        fw.barrier()
    return nc


def phase_consts(C):
    nc, fw = C.nc, C.fw
    with ExitStack() as st:
        fw.dma("pool", C.ident[:], C.cmat[:, 0, :], writes=[C.ident.b])
        fw.dma("sp", C.Jf[:], C.cmat[:, 1, :], writes=[C.Jf.b])
        fw.dma("pool", C.Rm[:], C.cmat[:, 2, :], writes=[C.Rm.b])
        fw.dma("pool", C.blk64[:], C.cmat[:, 3, :], writes=[C.blk64.b])
        fw.op("pool", lambda e: e.memset(C.E65[:], 0.0), writes=[C.E65.b])
        fw.op("pool", lambda e: e.memset(C.E65[64:65, :], 1.0), writes=[C.E65.b])
        fw.dma("sp", C.cfar[:, 0:4], C.rel_bias[15:16, 4:8].partition_broadcast(128), writes=[C.cfar.b], add=True)
        fw.dma("sp", C.cfar[:, 4:8], C.rel_bias[31:32, 4:8].partition_broadcast(128), writes=[C.cfar.b], add=True)
        fw.dma("sp", C.cfar[:, 8:12], C.rel_bias[15:16, 0:4].partition_broadcast(128), writes=[C.cfar.b], add=True)
        rbx = mktile(C, st, "rbx", [33, 16], F32)
        oh = mktile(C, st, "oh", [33, 512], F32)
        fsb = mktile(C, st, "fsb", [16, 512], F32)
        psf = mkpsum(C, st, "psf", [128, 512])
        fw.op("pool", lambda e: e.memset(rbx[:], 0.0), writes=[rbx.b])
        fw.op("pool", lambda e: e.memset(rbx[32:33, 0:4], -BIG), writes=[rbx.b])
        fw.dma("sp", rbx[0:32, 0:8], C.rel_bias[:, :], reads=[], writes=[rbx.b])
        fw.dma("sp", rbx[0:32, 8:12], C.rel_bias[:, 0:4], writes=[rbx.b], add=True)
        fw.dma("sp", oh[:], C.oh_tab[:, :], writes=[oh.b])
        fw.op("pe", lambda e: e.matmul(psf[0:16, :], lhsT=rbx[0:33, 0:16], rhs=oh[0:33, :], start=True, stop=True),
              reads=[rbx.b, oh.b], writes=[psf.b])
        fw.op("dve", lambda e: e.tensor_copy(out=fsb[:], in_=psf[0:16, :]), reads=[psf.b], writes=[fsb.b])
        fvb = Buf("fvec")
        fw.dma("sp", C.fvec[:, :], fsb[:], reads=[fsb.b], writes=[fvb])
        hk = [mktile(C, st, "hk%d" % i, [128, 128], F32) for i in range(2)]
        pst = [mkpsum(C, st, "pst%d" % i, [128, 512]) for i in range(2)]
        n = 0
        for var in range(3):
            for h in range(4):
                for off in (-1, 0, 1):
                    if var == 2 and off == 1:
                        continue
                    rowi = (0, 4, 8)[var] + h
                    cst = 128 * off + 129
                    src = bass.AP(C.fvec.tensor, rowi * 512 + cst, [[1, 128], [1, 128]])
                    hkt = hk[n % 2]
                    ps = pst[n % 2]
                    fw.dma("sp", hkt[:], src, reads=[fvb], writes=[hkt.b])
                    fw.op("pe", lambda e, hkt=hkt, ps=ps: e.matmul(ps[:, 0:128], lhsT=hkt[:], rhs=C.Jf[:], start=True, stop=True),
                          reads=[hkt.b, C.Jf.b], writes=[ps.b])
                    slot = var * 12 + h * 3 + (off + 1)
                    mul = (1.0 / SC_C) if var == 1 else (1.0 / SC_A)
                    fw.op("act", lambda e, ps=ps, slot=slot, mul=mul: e.mul(out=C.TbT[:, slot, :], in_=ps[:, 0:128], mul=mul),
                          reads=[ps.b], writes=[C.TbT.b], add=True)
                    n += 1
        fw.barrier()


def layer_params(C, l):
    nc, fw = C.nc, C.fw
    sm = C.small
    lam_init = 0.8 - 0.6 * math.exp(-0.3 * l)
    with ExitStack() as st:
        dl = mktile(C, st, "dl", [128, 128], F32)
        tmp = mktile(C, st, "ptmp", [128, 64], F32)
        lg = mktile(C, st, "lg", [128, 8], F32)
        fw.barrier()
        for hh in range(2):
            fw.dma("sp", sm[hh * 64:(hh + 1) * 64, 0:1], C.qk_q[l:l + 1, :].rearrange("o d -> d o"), writes=[sm.b], add=True)
            fw.dma("sp", sm[hh * 64:(hh + 1) * 64, 1:2], C.qk_k[l:l + 1, :].rearrange("o d -> d o"), writes=[sm.b], add=True)
        fw.dma("sp", sm[:, 2:6], C.attn_sink[l:l + 1, :].partition_broadcast(128), writes=[sm.b], add=True)
        fw.dma("sp", sm[0:64, 7:8], C.diff_subnorm[l:l + 1, :].rearrange("o d -> d o"), writes=[sm.b], add=True)
        fw.dma("sp", sm[0:64, 8:9], C.hgrn_out_norm[l:l + 1, :].rearrange("o d -> d o"), writes=[sm.b], add=True)
        fw.dma("sp", dl[:], C.diff_lambda[l:l + 1, :].partition_broadcast(128), writes=[dl.b])
        fw.op("act", lambda e: e.activation(out=sm[:, 2:6], in_=sm[:, 2:6], func=AF.Exp), reads=[sm.b], writes=[sm.b])
        fw.op("dve", lambda e: e.tensor_scalar(out=sm[0:64, 7:8], in0=sm[0:64, 7:8], scalar1=float(1.0 - lam_init), scalar2=None, op0=ALU.mult),
              reads=[sm.b], writes=[sm.b])
        fw.op("dve", lambda e: e.tensor_tensor(out=tmp[:, 0:32], in0=dl[:, 0:32], in1=dl[:, 32:64], op=ALU.mult), reads=[dl.b], writes=[tmp.b])
        fw.op("dve", lambda e: e.tensor_tensor(out=tmp[:, 32:64], in0=dl[:, 64:96], in1=dl[:, 96:128], op=ALU.mult), reads=[dl.b, tmp.b], writes=[tmp.b])
        fw.op("dve", lambda e: e.reduce_sum(out=sm[:, 9:10], in_=tmp[:, 0:32], axis=mybir.AxisListType.X), reads=[tmp.b], writes=[sm.b])
        fw.op("dve", lambda e: e.reduce_sum(out=sm[:, 10:11], in_=tmp[:, 32:64], axis=mybir.AxisListType.X), reads=[tmp.b, sm.b], writes=[sm.b])
        fw.op("act", lambda e: e.activation(out=sm[:, 9:11], in_=sm[:, 9:11], func=AF.Exp), reads=[sm.b], writes=[sm.b])
        fw.op("dve", lambda e: e.scalar_tensor_tensor(out=sm[:, 6:7], in0=sm[:, 10:11], scalar=float(-lam_init), in1=sm[:, 9:10],
                                                      op0=ALU.add, op1=ALU.subtract), reads=[sm.b], writes=[sm.b])
        if l == 0:
            fw.op("dve", lambda e: e.memset(C.lbt[:, :, 0:1], 0.0), writes=[C.lbt.b])
            fw.op("dve", lambda e: e.memset(C.lbt[:, :, 1:2], 1.0), reads=[C.lbt.b], writes=[C.lbt.b])
        else:
            for d in range(2):
                for hp in range(2):
                    for ly in range(2):
                        c = (d * 2 + hp) * 2 + ly
                        src = C.lb_logits[d, ly:ly + 1, hp * 128:(hp + 1) * 128].rearrange("o d -> d o")
                        fw.dma("sp", lg[:, c:c + 1], src, writes=[lg.b], add=True)
            lgv = lg[:].rearrange("p (c t) -> p c t", t=2)
            fw.op("dve", lambda e: e.tensor_tensor(out=C.lbt[:, :, 0:1], in0=lgv[:, :, 1:2], in1=lgv[:, :, 0:1], op=ALU.subtract),
                  reads=[lg.b], writes=[C.lbt.b])
            fw.op("act", lambda e: e.activation(out=C.lbt[:, :, 0:1], in_=C.lbt[:, :, 0:1], func=AF.Sigmoid), reads=[C.lbt.b], writes=[C.lbt.b])
            fw.op("dve", lambda e: e.tensor_scalar(out=C.lbt[:, :, 1:2], in0=C.lbt[:, :, 0:1], scalar1=-1.0, scalar2=1.0, op0=ALU.mult, op1=ALU.add),
                  reads=[C.lbt.b], writes=[C.lbt.b])
        fw.barrier()


def load_h_tile(C, l, t, ht, first):
    fw = C.fw
    if l == 0 and first:
        if t == 0:
            fw.op("pool", lambda e: e.memset(ht[:], 0.0), writes=[ht.b])
            fw.dma("sp", ht[FRONT:128, :], C.meta[:, :], reads=[], writes=[ht.b], add=True)
        else:
            fw.dma("sp", ht[:], C.x[(t - 1) * 128:t * 128, :], writes=[ht.b])
    else:
        fw.dma("sp", ht[:], C.Hd[t * 128:(t + 1) * 128, :], writes=[ht.b])


def norm_to_uT(C, ht, gB, hn, ss, ptr, uT, j):
    fw = C.fw
    fw.op("act", lambda e: e.activation(out=hn[:], in_=ht[:], func=AF.Square, accum_out=ss[:, 0:1]),
          reads=[ht.b], writes=[hn.b, ss.b])
    fw.op("act", lambda e: e.activation(out=ss[:, 1:2], in_=ss[:, 0:1], func=AF.Sqrt, bias=EPS, scale=1.0 / D_MODEL),
          reads=[ss.b], writes=[ss.b])
    fw.op("dve", lambda e: e.reciprocal(out=ss[:, 1:2], in_=ss[:, 1:2]), reads=[ss.b], writes=[ss.b])
    fw.op("dve", lambda e: e.scalar_tensor_tensor(out=hn[:], in0=ht[:], scalar=ss[:, 1:2], in1=gB[:], op0=ALU.mult, op1=ALU.mult),
          reads=[ht.b, ss.b, gB.b, hn.b], writes=[hn.b])

    def tr(e):
        for kc in range(8):
            i = e.transpose(out=ptr[:, kc, :], in_=hn[:, kc * 128:(kc + 1) * 128], identity=C.ident[:])
        return i
    fw.op("pe", tr, reads=[hn.b, C.ident.b], writes=[ptr.b])
    fw.op("dve", lambda e: e.tensor_copy(out=uT[:, :, j * 128:(j + 1) * 128], in_=ptr[:]), reads=[ptr.b], writes=[uT.b], add=(j > 0))


def load_weight_bf16(C, dst, src3, ncols, blk=1024, first=True):
    fw = C.fw
    if not hasattr(dst, "parts"):
        dst.parts = []
    c0 = 0
    while c0 < ncols:
        c1 = min(ncols, c0 + blk)
        b = Buf("wblk")
        fw.dma("pool", dst[:, :, c0:c1], src3[:, :, c0:c1], writes=[b])
        dst.parts.append((c0, c1, b))
        c0 = c1


def wb(W, c0, c1):
    return [b for (a0, a1, b) in W.parts if a0 < c1 and a1 > c0]


def phase_p1(C, l):
    nc, fw = C.nc, C.fw
    with ExitStack() as st:
        Wm = mktile(C, st, "Wm", [128, 8, NMIX], BF16)
        CS = mktile(C, st, "CS", [128, 2, L], F32)
        gB = mktile(C, st, "gBmix", [128, D_MODEL], F32)
        hts = [mktile(C, st, "ht%d" % i, [128, D_MODEL], F32) for i in range(2)]
        hn = mktile(C, st, "hn", [128, D_MODEL], BF16)
        sss = [mktile(C, st, "ss%d" % i, [128, 2], F32) for i in range(2)]
        uTs = [mktile(C, st, "uT%d" % i, [128, 8, G], BF16) for i in range(2)]
        Fo = [mktile(C, st, "Fo%d" % i, [128, G], BF16) for i in range(3)]
        Ff = [mktile(C, st, "Ff%d" % i, [128, G], F32) for i in range(2)]
        To = [mktile(C, st, "To%d" % i, [128, NTM], BF16) for i in range(2)]
        bsq = mktile(C, st, "bsq", [128, G], BF16)
        brs = mktile(C, st, "brs", [128, G], F32)
        bqn = mktile(C, st, "bqn", [128, G], BF16)
        bt1 = mktile(C, st, "bt1", [128, G], F32)
        bt2 = mktile(C, st, "bt2", [128, G], F32)
        ptr = mkpsum(C, st, "ptr", [128, 8, 128], BF16)
        psF = [mkpsum(C, st, "psF%d" % i, [128, 512]) for i in range(2)]
        psT1 = mkpsum(C, st, "psT1", [128, 512])
        psT2 = mkpsum(C, st, "psT2", [128, 512])
        psB1 = mkpsum(C, st, "psB1", [128, 512])
        psB2 = mkpsum(C, st, "psB2", [128, 512])

        load_weight_bf16(C, Wm, C.w_mix[l].rearrange("(kc p) c -> p kc c", p=128), NMIX, blk=736)
        fw.dma("sp", CS[:], C.cs_tab.rearrange("t p l -> p t l"), writes=[CS.b])
        fw.dma("sp", gB[:], C.ln_mix[l:l + 1, :].partition_broadcast(128), writes=[gB.b])

        nfo = 0
        for g in range(NG):
            t0 = g * G
            uT = uTs[g % 2]
            for j in range(TPG):
                t = g * TPG + j
                ht = hts[t % 2]
                load_h_tile(C, l, t, ht, True)
                norm_to_uT(C, ht, gB, hn, sss[t % 2], ptr, uT, j)
            fw.dma("pool", C.UTd.rearrange("k p l -> p k l")[:, :, t0:t0 + G], uT[:], reads=[uT.b])
            for fc in range(NF):
                ps = psF[fc % 2]

                def mm(e, ps=ps, fc=fc):
                    for kc in range(8):
                        i = e.matmul(ps[:, 0:G], lhsT=Wm[:, kc, fc * 128:(fc + 1) * 128], rhs=uT[:, kc, :],
                                     start=(kc == 0), stop=(kc == 7))
                    return i
                fw.op("pe", mm, reads=wb(Wm, fc * 128, (fc + 1) * 128) + [uT.b], writes=[ps.b])
                if fc in (3, 4, 5):
                    gcol = C.small[:, 0:1] if fc < 5 else C.small[:, 1:2]
                    fw.op("act", lambda e, ps=ps: e.activation(out=bsq[:], in_=ps[:, 0:G], func=AF.Square), reads=[ps.b], writes=[bsq.b])
                    fw.op("pe", lambda e: e.matmul(psB1[:, 0:G], lhsT=C.blk64[:], rhs=bsq[:], start=True, stop=True),
                          reads=[bsq.b, C.blk64.b], writes=[psB1.b])
                    fw.op("act", lambda e: e.activation(out=brs[:], in_=psB1[:, 0:G], func=AF.Sqrt, bias=EPS, scale=1.0 / 64),
                          reads=[psB1.b], writes=[brs.b])
                    fw.op("dve", lambda e: e.reciprocal(out=brs[:], in_=brs[:]), reads=[brs.b], writes=[brs.b])
                    fw.op("dve", lambda e, ps=ps, gcol=gcol: e.scalar_tensor_tensor(out=bqn[:], in0=ps[:, 0:G], scalar=gcol, in1=brs[:],
                                                                                     op0=ALU.mult, op1=ALU.mult),
                          reads=[ps.b, brs.b, C.small.b], writes=[bqn.b])
                    fw.op("pe", lambda e: e.matmul(psB2[:, 0:G], lhsT=C.Rm[:], rhs=bqn[:], start=True, stop=True),
                          reads=[bqn.b, C.Rm.b], writes=[psB2.b])
                    fw.op("pool", lambda e: e.tensor_tensor(out=bt1[:], in0=bqn[:], in1=CS[:, 0, t0:t0 + G], op=ALU.mult),
                          reads=[bqn.b, CS.b], writes=[bt1.b])
                    fw.op("dve", lambda e: e.tensor_tensor(out=bt2[:], in0=psB2[:, 0:G], in1=CS[:, 1, t0:t0 + G], op=ALU.mult),
                          reads=[psB2.b, CS.b], writes=[bt2.b])
                    fo = Fo[nfo % 3]
                    nfo += 1
                    fw.op("pool", lambda e, fo=fo: e.tensor_tensor(out=fo[:], in0=bt1[:], in1=bt2[:], op=ALU.add),
                          reads=[bt1.b, bt2.b], writes=[fo.b])
                    fw.dma("pool", C.Fd[fc, :, t0:t0 + G], fo[:], reads=[fo.b])
                elif fc in (12, 13, 14, 15):
                    c = fc - 12
                    ff = Ff[c % 2]
                    fw.op("act", lambda e, ps=ps, ff=ff: e.activation(out=ff[:], in_=ps[:, 0:G], func=AF.Sigmoid), reads=[ps.b], writes=[ff.b])
                    fw.op("dve", lambda e, ff=ff, c=c: e.tensor_scalar(out=ff[:], in0=ff[:], scalar1=C.lbt[:, c, 1:2], scalar2=C.lbt[:, c, 0:1],
                                                                       op0=ALU.mult, op1=ALU.add), reads=[ff.b, C.lbt.b], writes=[ff.b])
                    fw.op("dve", lambda e, ff=ff: e.tensor_scalar(out=ff[:], in0=ff[:], scalar1=1e-30, scalar2=None, op0=ALU.max),
                          reads=[ff.b], writes=[ff.b])
                    fw.op("act", lambda e, ff=ff: e.activation(out=ff[:], in_=ff[:], func=AF.Ln), reads=[ff.b], writes=[ff.b])
                    fw.dma("pool", C.LFd[c, :, t0:t0 + G], ff[:], reads=[ff.b])
                else:
                    fo = Fo[nfo % 3]
                    nfo += 1
                    if fc in (16, 17):
                        fw.op("act", lambda e, ps=ps, fo=fo: e.activation(out=fo[:], in_=ps[:, 0:G], func=AF.Silu), reads=[ps.b], writes=[fo.b])
                    elif fc in (10, 11):
                        fw.op("act", lambda e, ps=ps, fo=fo: e.mul(out=fo[:], in_=ps[:, 0:G], mul=0.125), reads=[ps.b], writes=[fo.b])
                    else:
                        fw.op("dve", lambda e, ps=ps, fo=fo: e.tensor_copy(out=fo[:], in_=ps[:, 0:G]), reads=[ps.b], writes=[fo.b])
                    fw.dma("pool", C.Fd[fc, :, t0:t0 + G], fo[:], reads=[fo.b])
            for j in range(TPG):
                t = g * TPG + j
                to = To[t % 2]

                def mmt(e, j=j):
                    for kc in range(8):
                        e.matmul(psT1[:, 0:512], lhsT=uT[:, kc, j * 128:(j + 1) * 128], rhs=Wm[:, kc, NF * 128:NF * 128 + 512],
                                 start=(kc == 0), stop=(kc == 7))
                    for kc in range(8):
                        i = e.matmul(psT2[:, 0:128], lhsT=uT[:, kc, j * 128:(j + 1) * 128], rhs=Wm[:, kc, NF * 128 + 512:NMIX],
                                     start=(kc == 0), stop=(kc == 7))
                    return i
                fw.op("pe", mmt, reads=wb(Wm, NF * 128, NMIX) + [uT.b], writes=[psT1.b, psT2.b])
                fw.op("act", lambda e, to=to: e.copy(out=to[:, 0:512], in_=psT1[:, 0:512]), reads=[psT1.b], writes=[to.b])
                fw.op("dve", lambda e, to=to: e.tensor_copy(out=to[:, 512:NTM], in_=psT2[:, 0:128]), reads=[psT2.b], writes=[to.b], add=True)
                fw.dma("pool", C.Td[t * 128:(t + 1) * 128, :], to[:], reads=[to.b])
        fw.barrier()


def finalize_attn(C, O_ps, n, O_sb, den_ps, rec, ysb_out, out_buf, add=False, sink_col=None):
    fw = C.fw
    fw.op("dve", lambda e: e.tensor_copy(out=O_sb[0:65, 0:n], in_=O_ps[0:65, 0:n]), reads=[O_ps.b], writes=[O_sb.b])
    fw.op("pe", lambda e: e.matmul(den_ps[0:64, 0:n], lhsT=C.E65[0:65, 0:64], rhs=O_sb[0:65, 0:n], start=True, stop=True),
          reads=[O_sb.b, C.E65.b], writes=[den_ps.b])
    if sink_col is not None:
        fw.op("dve", lambda e: e.tensor_scalar(out=rec[0:64, 0:n], in0=den_ps[0:64, 0:n], scalar1=sink_col, scalar2=None, op0=ALU.add),
              reads=[den_ps.b, C.small.b], writes=[rec.b])
        fw.op("dve", lambda e: e.reciprocal(out=rec[0:64, 0:n], in_=rec[0:64, 0:n]), reads=[rec.b], writes=[rec.b])
    else:
        fw.op("dve", lambda e: e.reciprocal(out=rec[0:64, 0:n], in_=den_ps[0:64, 0:n]), reads=[den_ps.b], writes=[rec.b])
    fw.op("dve", lambda e: e.tensor_tensor(out=ysb_out, in0=O_sb[0:64, 0:n], in1=rec[0:64, 0:n], op=ALU.mult),
          reads=[O_sb.b, rec.b], writes=[out_buf], add=add)


def load_vaug(C, Va, col0):
    fw = C.fw
    src = C.Td.rearrange("(b p) c -> p b c", p=128)
    first = True
    for b0 in range(0, NT, 11):
        for kv in range(2):
            fw.dma("sp", Va[:, b0:b0 + 11, kv, 0:64], src[:, b0:b0 + 11, col0 + kv * 64:col0 + kv * 64 + 64],
                   writes=[Va.b], add=not first)
            first = False
    fw.op("pool", lambda e: e.memset(Va[:, :, :, 64:65], 1.0), writes=[Va.b], add=True)
    fw.op("pool", lambda e: e.memset(Va[0:FRONT, 0, :, :], 0.0), reads=[Va.b], writes=[Va.b])


def phase_attn(C, l):
    nc, fw = C.nc, C.fw
    with ExitStack() as st:
        Q0 = mktile(C, st, "Q0", [128, L], BF16)
        Q1 = mktile(C, st, "Q1", [128, L], BF16)
        K0 = mktile(C, st, "K0", [128, L], BF16)
        K1 = mktile(C, st, "K1", [128, L], BF16)
        Va = mktile(C, st, "Va", [128, NT, 2, 65], BF16)
        Ps = [mktile(C, st, "Pe%d" % i, [128, 2, G], BF16) for i in range(2)]
        O_sb = [mktile(C, st, "Osb%d" % i, [128, G], F32) for i in range(2)]
        rec = mktile(C, st, "rec", [128, G], F32)
        y0 = mktile(C, st, "y0", [64, G], F32)
        y1 = mktile(C, st, "y1", [64, G], F32)
        yc = mktile(C, st, "yc", [64, G], F32)
        ysq = mktile(C, st, "ysq", [64, G], BF16)
        yr = mktile(C, st, "yr", [64, G], F32)
        Yo = [mktile(C, st, "Yo%d" % i, [64, G], BF16) for i in range(2)]
        Sps = [mkpsum(C, st, "Sps%d" % i, [128, 1024]) for i in range(2)]
        Ops = [mkpsum(C, st, "Ops%d" % i, [128, 512]) for i in range(2)]
        den_ps = mkpsum(C, st, "denps", [128, 512])
        ss_ps = mkpsum(C, st, "ssps", [128, 512])
        Qs = [Q0, Q1]
        nyo = 0

        def loadqk(base, kc_list):
            fw.dma("sp", Q0[:], C.Fd[base, :, :], writes=[Q0.b])
            fw.dma("sp", Q1[:], C.Fd[base + 1, :, :], writes=[Q1.b])
            fw.dma("sp", K0[:], C.Fd[kc_list[0], :, :], writes=[K0.b])
            if len(kc_list) > 1:
                fw.dma("sp", K1[:], C.Fd[kc_list[1], :, :], writes=[K1.b])

        def run_contexts(ctxs):
            flat = [(c, si) for c in ctxs for si in range(len(c["steps"]))]
            fins = []

            def emitS(k):
                c, si = flat[k]
                c["steps"][si]["S"](Sps[k % 2])
                return k % 2
            cur = emitS(0)
            for k in range(len(flat)):
                nxt = emitS(k + 1) if k + 1 < len(flat) else None
                c, si = flat[k]
                st_ = c["steps"][si]
                st_["E"](Sps[cur], Ps[cur])
                st_["PV"](Ps[cur], si == 0, si == len(c["steps"]) - 1)
                while fins and fins[0][0] <= k:
                    fins.pop(0)[1]()
                if si == len(c["steps"]) - 1:
                    fins.append((k + 2, c["fin"]))
                cur = nxt
            for _, f in fins:
                f()

        loadqk(0, [2])
        load_vaug(C, Va, 0)
        ctxs = []
        nctx = 0
        for g in range(NG):
            for kv in range(2):
                for gg in range(2):
                    h = kv * 2 + gg
                    Q = Qs[gg]
                    yo = Yo[nyo % 2]
                    nyo += 1
                    for j3 in range(TPG):
                        i = g * TPG + j3
                        O_ps = Ops[nctx % 2]
                        Osb = O_sb[nctx % 2]
                        nctx += 1
                        blocks = [("meta", 0)] + [("band", j) for j in (i - 1, i, i + 1) if 1 <= j <= NT - 1]
                        steps = []
                        for (kind, j) in blocks:
                            off = j - i
                            if kind == "band":
                                slot = 0 * 12 + h * 3 + (off + 1)
                            elif i <= 1:
                                slot = 2 * 12 + h * 3 + (off + 1)
                            else:
                                slot = None

                            def S(S_ps, j=j, i=i, kv=kv, Q=Q, slot=slot):
                                def mm(e):
                                    r = e.matmul(S_ps[:, 0:128], lhsT=K0[kv * 64:(kv + 1) * 64, j * 128:(j + 1) * 128],
                                                 rhs=Q[kv * 64:(kv + 1) * 64, i * 128:(i + 1) * 128], start=True, stop=(slot is None))
                                    if slot is not None:
                                        r = e.matmul(S_ps[:, 0:128], lhsT=C.ident[:], rhs=C.TbT[:, slot, :], start=False, stop=True)
                                    return r
                                fw.op("pe", mm, reads=[K0.b, Q.b, C.ident.b, C.TbT.b], writes=[S_ps.b])

                            def E(S_ps, P_sb, slot=slot, h=h):
                                if slot is None:
                                    fw.op("act", lambda e: e.activation(out=P_sb[:, 0, 0:128], in_=S_ps[:, 0:128], func=AF.Exp,
                                                                        bias=C.cfar[:, 8 + h:9 + h], scale=SC_A),
                                          reads=[S_ps.b, C.cfar.b], writes=[P_sb.b])
                                else:
                                    fw.op("act", lambda e: e.activation(out=P_sb[:, 0, 0:128], in_=S_ps[:, 0:128], func=AF.Exp, scale=SC_A),
                                          reads=[S_ps.b], writes=[P_sb.b])

                            def PV(P_sb, first, lastf, O_ps=O_ps, j=j, kv=kv):
                                fw.op("pe", lambda e: e.matmul(O_ps[0:65, 0:128], lhsT=Va[:, j, kv, 0:65], rhs=P_sb[:, 0, 0:128], start=first, stop=lastf),
                                      reads=[Va.b, P_sb.b], writes=[O_ps.b])
                            steps.append({"S": S, "E": E, "PV": PV})

                        def fin(O_ps=O_ps, Osb=Osb, yo=yo, j3=j3, h=h, g=g):
                            finalize_attn(C, O_ps, 128, Osb, den_ps, rec, yo[:, j3 * 128:(j3 + 1) * 128], yo.b, add=(j3 > 0),
                                          sink_col=C.small[0:64, 2 + h:3 + h])
                            if j3 == TPG - 1:
                                fw.dma("pool", C.Yd[0 + h, :, g * G:(g + 1) * G], yo[:], reads=[yo.b])
                        ctxs.append({"steps": steps, "fin": fin})
        run_contexts(ctxs)

        for mixer in ("B", "C"):
            if mixer == "B":
                loadqk(3, [5])
                load_vaug(C, Va, 128)
                comps = [None]
                scale = SC_A
            else:
                loadqk(6, [8, 9])
                load_vaug(C, Va, 256)
                comps = [0, 1]
                scale = SC_C
            ctxs = []
            for g in range(NG):
                q0 = g * G
                for kv in range(2):
                    for gg in range(2):
                        h = kv * 2 + gg
                        Q = Qs[gg]
                        for ci, comp in enumerate(comps):
                            K = K0 if comp in (None, 0) else K1
                            O_ps = Ops[nctx % 2]
                            Osb = O_sb[nctx % 2]
                            nctx += 1
                            steps = []
                            if mixer == "B":
                                blist = [[j] for j in range(NT)]
                                units = [sum(blist[a:a + 2], []) for a in range(0, NT, 2)]
                            else:
                                nearb = [j for j in range(NT) if any(abs(g * TPG + i3 - j) <= 1 for i3 in range(TPG))]
                                left = [j for j in range(NT) if j < nearb[0]]
                                right = [j for j in range(NT) if j > nearb[-1]]
                                units = [left[a:a + 2] for a in range(0, len(left), 2)] + [[j] for j in nearb] + \
                                        [right[a:a + 2] for a in range(0, len(right), 2)]
                            nun = len(units)
                            for ui, ub in enumerate(units):
                                near = []
                                if mixer == "C" and len(ub) == 1:
                                    near = [i3 for i3 in range(TPG) if abs(g * TPG + i3 - ub[0]) <= 1]

                                def S(S_ps, K=K, Q=Q, kv=kv, ub=ub, q0=q0, near=near, g=g, h=h):
                                    def mm(e):
                                        for bi, j in enumerate(ub):
                                            r = e.matmul(S_ps[:, bi * 512:bi * 512 + G], lhsT=K[kv * 64:(kv + 1) * 64, j * 128:(j + 1) * 128],
                                                         rhs=Q[kv * 64:(kv + 1) * 64, q0:q0 + G], start=True, stop=(len(near) == 0))
                                        for ni, i3 in enumerate(near):
                                            off = ub[0] - (g * TPG + i3)
                                            slot = 12 + h * 3 + (off + 1)
                                            r = e.matmul(S_ps[:, i3 * 128:(i3 + 1) * 128], lhsT=C.ident[:], rhs=C.TbT[:, slot, :],
                                                         start=False, stop=(ni == len(near) - 1))
                                        return r
                                    fw.op("pe", mm, reads=[K.b, Q.b, C.ident.b, C.TbT.b], writes=[S_ps.b])

                                def E(S_ps, P_sb, mixer=mixer, near=near, ub=ub, g=g, h=h, scale=scale):
                                    nb = len(ub)
                                    if nb == 2:
                                        src = S_ps[:, 0:1024].rearrange("p (b c) -> p b c", b=2)[:, :, 0:G]
                                        dst = P_sb[:, :, :]
                                    else:
                                        src = S_ps[:, 0:G]
                                        dst = P_sb[:, 0, :]
                                    if mixer == "B":
                                        fw.op("act", lambda e: e.activation(out=dst, in_=src, func=AF.Exp, scale=scale),
                                              reads=[S_ps.b], writes=[P_sb.b])
                                        return
                                    j = ub[0]

                                    def sidecol(i3):
                                        return C.cfar[:, h:h + 1] if j < g * TPG + i3 else C.cfar[:, 4 + h:5 + h]
                                    if not near:
                                        bcol = sidecol(0)
                                        fw.op("act", lambda e: e.activation(out=dst, in_=src, func=AF.Exp, bias=bcol, scale=scale),
                                              reads=[S_ps.b, C.cfar.b], writes=[P_sb.b])
                                        return
                                    for i3 in range(TPG):
                                        sl = slice(i3 * 128, (i3 + 1) * 128)
                                        if i3 in near:
                                            fw.op("act", lambda e, sl=sl: e.activation(out=P_sb[:, 0, sl], in_=S_ps[:, sl], func=AF.Exp, scale=scale),
                                                  reads=[S_ps.b], writes=[P_sb.b], add=(i3 > 0))
                                        else:
                                            bcol = sidecol(i3)
                                            fw.op("act", lambda e, sl=sl, bcol=bcol: e.activation(out=P_sb[:, 0, sl], in_=S_ps[:, sl], func=AF.Exp,
                                                                                                 bias=bcol, scale=scale),
                                                  reads=[S_ps.b, C.cfar.b], writes=[P_sb.b], add=(i3 > 0))

                                def PV(P_sb, first, lastf, O_ps=O_ps, ub=ub, kv=kv):
                                    def mm(e):
                                        for bi, j in enumerate(ub):
                                            r = e.matmul(O_ps[0:65, 0:G], lhsT=Va[:, j, kv, 0:65], rhs=P_sb[:, bi, :],
                                                         start=(first and bi == 0), stop=(lastf and bi == len(ub) - 1))
                                        return r
                                    fw.op("pe", mm, reads=[Va.b, P_sb.b], writes=[O_ps.b])
                                steps.append({"S": S, "E": E, "PV": PV})

                            if mixer == "B":
                                def fin(O_ps=O_ps, Osb=Osb, h=h, q0=q0):
                                    nonlocal nyo
                                    yo = Yo[nyo % 2]
                                    nyo += 1
                                    finalize_attn(C, O_ps, G, Osb, den_ps, rec, yo[:, :], yo.b)
                                    fw.dma("pool", C.Yd[4 + h, :, q0:q0 + G], yo[:], reads=[yo.b])
                            else:
                                def fin(O_ps=O_ps, Osb=Osb, h=h, q0=q0, ci=ci):
                                    nonlocal nyo
                                    yt = (y0, y1)[ci]
                                    finalize_attn(C, O_ps, G, Osb, den_ps, rec, yt[:, :], yt.b)
                                    if ci == 0:
                                        return
                                    yo = Yo[nyo % 2]
                                    nyo += 1
                                    fw.op("dve", lambda e: e.scalar_tensor_tensor(out=yc[:], in0=y1[:], scalar=C.small[0:64, 6:7], in1=y0[:], op0=ALU.mult, op1=ALU.add),
                                          reads=[y0.b, y1.b, C.small.b], writes=[yc.b])
                                    fw.op("act", lambda e: e.activation(out=ysq[:], in_=yc[:], func=AF.Square), reads=[yc.b], writes=[ysq.b])
                                    fw.op("pe", lambda e: e.matmul(ss_ps[0:64, 0:G], lhsT=C.blk64[0:64, 0:64], rhs=ysq[:], start=True, stop=True),
                                          reads=[ysq.b, C.blk64.b], writes=[ss_ps.b])
                                    fw.op("act", lambda e: e.activation(out=yr[:], in_=ss_ps[0:64, 0:G], func=AF.Sqrt, bias=EPS, scale=1.0 / 64),
                                          reads=[ss_ps.b], writes=[yr.b])
                                    fw.op("dve", lambda e: e.reciprocal(out=yr[:], in_=yr[:]), reads=[yr.b], writes=[yr.b])
                                    fw.op("dve", lambda e: e.scalar_tensor_tensor(out=yo[:], in0=yc[:], scalar=C.small[0:64, 7:8], in1=yr[:], op0=ALU.mult, op1=ALU.mult),
                                          reads=[yc.b, yr.b, C.small.b], writes=[yo.b])
                                    fw.dma("pool", C.Yd[8 + h, :, q0:q0 + G], yo[:], reads=[yo.b])
                            ctxs.append({"steps": steps, "fin": fin})
            run_contexts(ctxs)
        fw.barrier()


def phase_hgrn(C, l):
    nc, fw = C.nc, C.fw
    with ExitStack() as st:
        lf = mktile(C, st, "lf", [64, L], F32)
        kk = mktile(C, st, "kk", [64, L], F32)
        bb = mktile(C, st, "bb", [64, L], F32)
        ex = mktile(C, st, "ex", [64, L], F32)
        rs = mktile(C, st, "rs", [64, L], F32)
        qd = mktile(C, st, "qd", [64, L], BF16)
        qt = [mktile(C, st, "qt%d" % d, [64, L], BF16) for d in range(2)]
        kt = [mktile(C, st, "kt%d" % d, [64, L], BF16) for d in range(2)]
        edec = [mktile(C, st, "edec%d" % d, [64, NCH], F32) for d in range(2)]
        Vc = mktile(C, st, "Vc", [64, NCH, 64], BF16)
        oo = [mktile(C, st, "oo%d" % d, [64, L], F32) for d in range(2)]
        hm = mktile(C, st, "hm", [64, 2, 64], BF16)
        S32 = [mktile(C, st, "S32_%d" % d, [64, 64], F32) for d in range(2)]
        S16 = [mktile(C, st, "S16_%d" % d, [64, 64], BF16) for d in range(2)]
        stmp = [mktile(C, st, "stmp%d" % d, [64, 64], F32) for d in range(2)]
        at_sb = [mktile(C, st, "atsb%d" % d, [64, 64], BF16) for d in range(2)]
        ktm_sb = [mktile(C, st, "ktm%d" % d, [64, 64], BF16) for d in range(2)]
        sgd = mktile(C, st, "sgd", [64, L], BF16)
        osq = mktile(C, st, "osq", [64, G], BF16)
        orr = mktile(C, st, "orr", [64, G], F32)
        Yo = [mktile(C, st, "YoD%d" % i, [64, G], BF16) for i in range(2)]
        at_ps = [mkpsum(C, st, "atps%d" % d, [128, 512]) for d in range(2)]
        kt_ps = [mkpsum(C, st, "ktps%d" % d, [128, 1024], BF16) for d in range(2)]
        o_ps = [mkpsum(C, st, "ops%d" % d, [128, 512]) for d in range(2)]
        u_ps = [mkpsum(C, st, "ups%d" % d, [128, 512]) for d in range(2)]

        fw.dma("sp", rs[:], C.scan_rs[:, :], writes=[rs.b])
        fw.dma("pool", hm[:], C.hmask[:, :, :], writes=[hm.b])
        for h in range(4):
            hp, hh = h // 2, h % 2
            r0 = hh * 64
            fw.dma("sp", qd[:], C.Fd[10 + hp, r0:r0 + 64, :], writes=[qd.b])
            fw.dma("sp", sgd[:], C.Fd[16 + hp, r0:r0 + 64, :], writes=[sgd.b])
            tdv = C.Td.rearrange("(c s) v -> s c v", s=CH)
            for c0 in range(0, NCH, 22):
                fw.dma("sp", Vc[:, c0:c0 + 22, :], tdv[:, c0:c0 + 22, 384 + h * 64:384 + h * 64 + 64], writes=[Vc.b], add=(c0 > 0))
            for d in range(2):
                fw.dma("sp", lf[:], C.LFd[d * 2 + hp, r0:r0 + 64, :], writes=[lf.b])
                fw.op("act", lambda e: e.activation(out=kk[:], in_=lf[:], func=AF.Exp), reads=[lf.b], writes=[kk.b])
                fw.op("dve", lambda e: e.tensor_scalar(out=kk[:], in0=kk[:], scalar1=-1.0, scalar2=1.0, op0=ALU.mult, op1=ALU.add),
                      reads=[kk.b], writes=[kk.b])
                fw.op("dve", lambda e: e.memset(kk[:, 0:FRONT], 0.0), reads=[kk.b], writes=[kk.b])
                fw.op("dve", lambda e: e.tensor_tensor_scan(out=bb[:], data0=rs[:], data1=lf[:], initial=0.0, op0=ALU.mult, op1=ALU.add),
                      reads=[rs.b, lf.b], writes=[bb.b])
                b3 = bb[:].rearrange("p (c t) -> p c t", t=CH)
                l3 = lf[:].rearrange("p (c t) -> p c t", t=CH)
                bs = bb
                if d == 1:
                    fw.op("dve", lambda e: e.tensor_tensor(out=ex[:], in0=lf[:], in1=bb[:], op=ALU.subtract), reads=[lf.b, bb.b], writes=[ex.b])
                    e3 = ex[:].rearrange("p (c t) -> p c t", t=CH)
                    fw.op("dve", lambda e: e.tensor_tensor(out=l3, in0=e3, in1=b3[:, :, CH - 1:CH].to_broadcast([64, NCH, CH]), op=ALU.add),
                          reads=[ex.b, bb.b], writes=[lf.b])
                    bs = lf
                    b3 = l3
                if d == 0:
                    fw.op("act", lambda e: e.activation(out=edec[0][:], in_=b3[:, :, CH - 1], func=AF.Exp), reads=[bs.b], writes=[edec[0].b])
                else:
                    fw.op("act", lambda e: e.activation(out=edec[1][:], in_=b3[:, :, 0], func=AF.Exp), reads=[bs.b], writes=[edec[1].b])
                fw.op("act", lambda e: e.activation(out=ex[:], in_=bs[:], func=AF.Exp), reads=[bs.b], writes=[ex.b])
                fw.op("dve", lambda e, d=d: e.tensor_tensor(out=qt[d][:], in0=qd[:], in1=ex[:], op=ALU.mult), reads=[qd.b, ex.b], writes=[qt[d].b])
                fw.op("act", lambda e: e.activation(out=ex[:], in_=bs[:], func=AF.Exp, scale=-1.0), reads=[bs.b], writes=[ex.b])
                fw.op("dve", lambda e, d=d: e.tensor_tensor(out=kt[d][:], in0=kk[:], in1=ex[:], op=ALU.mult), reads=[kk.b, ex.b], writes=[kt[d].b])
                fw.op("dve", lambda e, d=d: e.memset(S32[d][:], 0.0), writes=[S32[d].b])
                fw.op("dve", lambda e, d=d: e.memset(S16[d][:], 0.0), writes=[S16[d].b])
            for step in range(NCH):
                for d in range(2):
                    c = step if d == 0 else NCH - 1 - step
                    cs = slice(c * CH, (c + 1) * CH)
                    fw.op("pe", lambda e, d=d, cs=cs: e.matmul(at_ps[d][0:64, 0:64], lhsT=kt[d][:, cs], rhs=qt[d][:, cs], start=True, stop=True),
                          reads=[kt[d].b, qt[d].b], writes=[at_ps[d].b])
                    fw.op("pe", lambda e, d=d, cs=cs: e.transpose(out=kt_ps[d][0:64, 0:64], in_=kt[d][:, cs], identity=C.ident[0:64, 0:64]),
                          reads=[kt[d].b, C.ident.b], writes=[kt_ps[d].b])
                    fw.op("dve", lambda e, d=d: e.tensor_tensor(out=at_sb[d][:], in0=at_ps[d][0:64, 0:64], in1=hm[:, d, :], op=ALU.mult),
                          reads=[at_ps[d].b, hm.b], writes=[at_sb[d].b])
                    fw.op("act", lambda e, d=d: e.copy(out=ktm_sb[d][:], in_=kt_ps[d][0:64, 0:64]), reads=[kt_ps[d].b], writes=[ktm_sb[d].b])

                    def mmo(e, d=d, c=c, cs=cs):
                        e.matmul(o_ps[d][0:64, 0:64], lhsT=Vc[:, c, :], rhs=at_sb[d][:], start=True, stop=False)
                        return e.matmul(o_ps[d][0:64, 0:64], lhsT=S16[d][:], rhs=qt[d][:, cs], start=False, stop=True)
                    fw.op("pe", mmo, reads=[Vc.b, at_sb[d].b, S16[d].b, qt[d].b], writes=[o_ps[d].b])
                    fw.op("pe", lambda e, d=d, c=c: e.matmul(u_ps[d][0:64, 0:64], lhsT=ktm_sb[d][:], rhs=Vc[:, c, :], start=True, stop=True),
                          reads=[ktm_sb[d].b, Vc.b], writes=[u_ps[d].b])
                    fw.op("pool" if False else "act", lambda e, d=d, cs=cs: e.copy(out=oo[d][:, cs], in_=o_ps[d][0:64, 0:64]),
                          reads=[o_ps[d].b], writes=[oo[d].b], add=True)
                    fw.op("dve", lambda e, d=d: e.tensor_tensor(out=stmp[d][:], in0=u_ps[d][0:64, 0:64], in1=S32[d][:], op=ALU.add),
                          reads=[u_ps[d].b, S32[d].b], writes=[stmp[d].b])
                    fw.op("dve", lambda e, d=d, c=c: e.tensor_scalar(out=S32[d][:], in0=stmp[d][:], scalar1=edec[d][:, c:c + 1], scalar2=None, op0=ALU.mult),
                          reads=[stmp[d].b, edec[d].b], writes=[S32[d].b])
                    fw.op("act", lambda e, d=d: e.copy(out=S16[d][:], in_=S32[d][:]), reads=[S32[d].b], writes=[S16[d].b])
            fw.op("dve", lambda e: e.tensor_tensor(out=oo[0][:], in0=oo[0][:], in1=oo[1][:], op=ALU.add), reads=[oo[0].b, oo[1].b], writes=[oo[0].b])
            for g in range(NG):
                gs = slice(g * G, (g + 1) * G)
                yo = Yo[g % 2]
                ssp = at_ps[g % 2]
                fw.op("act", lambda e, gs=gs: e.activation(out=osq[:], in_=oo[0][:, gs], func=AF.Square), reads=[oo[0].b], writes=[osq.b])
                fw.op("pe", lambda e, ssp=ssp: e.matmul(ssp[0:64, 0:G], lhsT=C.blk64[0:64, 0:64], rhs=osq[:], start=True, stop=True),
                      reads=[osq.b, C.blk64.b], writes=[ssp.b])
                fw.op("act", lambda e, ssp=ssp: e.activation(out=orr[:], in_=ssp[0:64, 0:G], func=AF.Sqrt, bias=EPS, scale=1.0 / 64),
                      reads=[ssp.b], writes=[orr.b])
                fw.op("dve", lambda e: e.reciprocal(out=orr[:], in_=orr[:]), reads=[orr.b], writes=[orr.b])
                fw.op("dve", lambda e, gs=gs: e.scalar_tensor_tensor(out=orr[:], in0=oo[0][:, gs], scalar=C.small[0:64, 8:9], in1=orr[:], op0=ALU.mult, op1=ALU.mult),
                      reads=[oo[0].b, orr.b, C.small.b], writes=[orr.b])
                fw.op("dve", lambda e, gs=gs, yo=yo: e.tensor_tensor(out=yo[:], in0=orr[:], in1=sgd[:, gs], op=ALU.mult),
                      reads=[orr.b, sgd.b], writes=[yo.b])
                fw.dma("pool", C.Yd[12 + h, :, gs], yo[:], reads=[yo.b])
        fw.barrier()


def phase_p3a(C, l):
    nc, fw = C.nc, C.fw
    with ExitStack() as st:
        Wgz = mktile(C, st, "Wgz", [128, 8, 4096], BF16)
        Wb = mktile(C, st, "Wb", [64, 16, D_MODEL], BF16)
        Wo = mktile(C, st, "Wo", [128, 8, D_MODEL], BF16)
        uT = mktile(C, st, "uTa", [128, 8, G], BF16)
        Ysb = mktile(C, st, "Ysb", [64, 16, G], BF16)
        gate = [mktile(C, st, "gate%d" % i, [128, G], F32) for i in range(2)]
        merged = mktile(C, st, "merged", [128, G], F32)
        mtmp = [mktile(C, st, "mtmp%d" % i, [128, G], F32) for i in range(2)]
        mT = mktile(C, st, "mT", [128, 8, G], BF16)
        hts = [mktile(C, st, "hta%d" % i, [128, D_MODEL], F32) for i in range(2)]
        psG = [mkpsum(C, st, "psG%d" % i, [128, 512]) for i in range(2)]
        psP = [mkpsum(C, st, "psP%d" % i, [128, 512]) for i in range(2)]
        psO = [mkpsum(C, st, "psO%d" % i, [128, 512]) for i in range(2)]

        gsrc = C.w_gz[l].rearrange("(kc p) (n cc c) -> p kc cc n c", p=128, n=4, cc=8)
        wbsrc = C.w_branch[l].rearrange("n (h p) c -> p (n h) c", p=64)
        Wgz.parts = []
        Wb.parts = []

        gflat = C.w_gz[l].rearrange("(kc p) c -> p kc c", p=128)

        def ld_gz(cc0):
            b = Buf("wgz")
            for cc in (cc0, cc0 + 1):
                for n in range(4):
                    fw.dma("pool", Wgz[:, :, cc * 512 + n * 128:cc * 512 + (n + 1) * 128],
                           gflat[:, :, n * 1024 + cc * 128:n * 1024 + (cc + 1) * 128], writes=[b], add=True)
            Wgz.parts.append((cc0 * 512, (cc0 + 2) * 512, b))

        def ld_wb(c0):
            b = Buf("wbr")
            fw.dma("pool", Wb[:, :, c0:c0 + 512], wbsrc[:, :, c0:c0 + 512], writes=[b])
            Wb.parts.append((c0, c0 + 512, b))
        ld_gz(0)
        ld_wb(0)
        ld_gz(2)
        ld_gz(4)
        ld_wb(512)
        ld_gz(6)
        load_weight_bf16(C, Wo, C.w_out[l].rearrange("(kc p) c -> p kc c", p=128), D_MODEL, blk=512)
        nt = 0
        for g in range(NG):
            t0 = g * G
            fw.dma("sp", uT[:], C.UTd.rearrange("k p l -> p k l")[:, :, t0:t0 + G], writes=[uT.b])
            fw.dma("sp", Ysb[:], C.Yd.rearrange("s p l -> p s l")[:, :, t0:t0 + G], writes=[Ysb.b])
            for cc in range(8):
                for n in range(4):
                    pg = psG[n % 2]
                    pp = psP[n % 2]
                    gt = gate[n % 2]

                    def mmg(e, pg=pg, n=n, cc=cc):
                        for kc in range(8):
                            i = e.matmul(pg[:, 0:G], lhsT=Wgz[:, kc, cc * 512 + n * 128:cc * 512 + (n + 1) * 128], rhs=uT[:, kc, :],
                                         start=(kc == 0), stop=(kc == 7))
                        return i
                    fw.op("pe", mmg, reads=wb(Wgz, cc * 512, (cc + 1) * 512) + [uT.b], writes=[pg.b])

                    def mmp(e, pp=pp, n=n, cc=cc):
                        for hh in range(4):
                            i = e.matmul(pp[:, 0:G], lhsT=Wb[0:64, n * 4 + hh, cc * 128:(cc + 1) * 128], rhs=Ysb[0:64, n * 4 + hh, :],
                                         start=(hh == 0), stop=(hh == 3))
                        return i
                    fw.op("pe", mmp, reads=wb(Wb, cc * 128, (cc + 1) * 128) + [Ysb.b], writes=[pp.b])
                    fw.op("act", lambda e, pg=pg, gt=gt: e.activation(out=gt[:], in_=pg[:, 0:G], func=AF.Sigmoid), reads=[pg.b], writes=[gt.b])
                    if n == 0:
                        fw.op("dve", lambda e, pp=pp, gt=gt: e.tensor_tensor(out=merged[:], in0=pp[:, 0:G], in1=gt[:], op=ALU.mult),
                              reads=[pp.b, gt.b], writes=[merged.b])
                    else:
                        mt = mtmp[n % 2]
                        fw.op("dve", lambda e, pp=pp, gt=gt, mt=mt: e.tensor_tensor(out=mt[:], in0=pp[:, 0:G], in1=gt[:], op=ALU.mult),
                              reads=[pp.b, gt.b], writes=[mt.b])
                        if n < 3:
                            fw.op("pool", lambda e, mt=mt: e.tensor_tensor(out=merged[:], in0=merged[:], in1=mt[:], op=ALU.add),
                                  reads=[merged.b, mt.b], writes=[merged.b])
                        else:
                            fw.op("pool", lambda e, mt=mt, cc=cc: e.tensor_tensor(out=mT[:, cc, :], in0=merged[:], in1=mt[:], op=ALU.add),
                                  reads=[merged.b, mt.b], writes=[mT.b], add=(cc > 0))
            for j in range(TPG):
                t = g * TPG + j
                ht = hts[t % 2]
                load_h_tile(C, l, t, ht, True)
                for half in range(2):
                    po = psO[half]

                    def mmo(e, po=po, j=j, half=half):
                        for cc in range(8):
                            i = e.matmul(po[:, 0:512], lhsT=mT[:, cc, j * 128:(j + 1) * 128], rhs=Wo[:, cc, half * 512:(half + 1) * 512],
                                         start=(cc == 0), stop=(cc == 7))
                        return i
                    fw.op("pe", mmo, reads=[mT.b] + wb(Wo, half * 512, (half + 1) * 512), writes=[po.b])
                    fw.op("dve", lambda e, po=po, ht=ht, half=half: e.tensor_tensor(out=ht[:, half * 512:(half + 1) * 512], in0=po[:, 0:512],
                                                                                   in1=ht[:, half * 512:(half + 1) * 512], op=ALU.add),
                          reads=[po.b, ht.b], writes=[ht.b])
                fw.dma("pool", C.Hd[t * 128:(t + 1) * 128, :], ht[:], reads=[ht.b])
        fw.barrier()


def phase_p3b(C, l, last):
    nc, fw = C.nc, C.fw
    with ExitStack() as st:
        Wg = mktile(C, st, "Wg", [128, 8, D_FF], BF16)
        Wu = mktile(C, st, "Wu", [128, 8, D_FF], BF16)
        Wd = mktile(C, st, "Wd", [128, NFF, D_MODEL], BF16)
        gB = mktile(C, st, "gBffn", [128, D_MODEL], F32)
        gF = mktile(C, st, "gBfin", [128, D_MODEL], F32)
        hts = [mktile(C, st, "htb%d" % i, [128, D_MODEL], F32) for i in range(TPG)]
        hn = mktile(C, st, "hnb", [128, D_MODEL], BF16)
        sss = [mktile(C, st, "ssb%d" % i, [128, 2], F32) for i in range(2)]
        uT = mktile(C, st, "uTb", [128, 8, G], BF16)
        aT = mktile(C, st, "aT", [128, NFF, G], BF16)
        sg = [mktile(C, st, "sg%d" % i, [128, G], F32) for i in range(2)]
        ptr = mkpsum(C, st, "ptrb", [128, 8, 128], BF16)
        psG = [mkpsum(C, st, "psGb%d" % i, [128, 512]) for i in range(2)]
        psU = [mkpsum(C, st, "psUb%d" % i, [128, 512]) for i in range(2)]
        psO = [mkpsum(C, st, "psOb%d" % i, [128, 512]) for i in range(2)]

        gs_ = C.w_g[l].rearrange("(kc p) c -> p kc c", p=128)
        us_ = C.w_u[l].rearrange("(kc p) c -> p kc c", p=128)
        Wg.parts = []
        Wu.parts = []
        for c0 in range(0, D_FF, 704):
            for W_, src_ in ((Wg, gs_), (Wu, us_)):
                b = Buf("wffn")
                fw.dma("pool", W_[:, :, c0:c0 + 704], src_[:, :, c0:c0 + 704], writes=[b])
                W_.parts.append((c0, c0 + 704, b))
        load_weight_bf16(C, Wd, C.w_d[l].rearrange("(kc p) c -> p kc c", p=128), D_MODEL, blk=512)
        fw.dma("sp", gB[:], C.ln_ffn[l:l + 1, :].partition_broadcast(128), writes=[gB.b])
        if last:
            fw.dma("sp", gF[:], C.ln_final[0:1, :].partition_broadcast(128), writes=[gF.b])
        for g in range(NG):
            for j in range(TPG):
                t = g * TPG + j
                load_h_tile(C, l, t, hts[j], False)
                norm_to_uT(C, hts[j], gB, hn, sss[t % 2], ptr, uT, j)
            for fc in range(NFF):
                pg = psG[fc % 2]
                pu = psU[fc % 2]
                s = sg[fc % 2]

                def mmg(e, pg=pg, fc=fc):
                    for kc in range(8):
                        i = e.matmul(pg[:, 0:G], lhsT=Wg[:, kc, fc * 128:(fc + 1) * 128], rhs=uT[:, kc, :], start=(kc == 0), stop=(kc == 7))
                    return i

                def mmu(e, pu=pu, fc=fc):
                    for kc in range(8):
                        i = e.matmul(pu[:, 0:G], lhsT=Wu[:, kc, fc * 128:(fc + 1) * 128], rhs=uT[:, kc, :], start=(kc == 0), stop=(kc == 7))
                    return i
                fw.op("pe", mmg, reads=wb(Wg, fc * 128, (fc + 1) * 128) + [uT.b], writes=[pg.b])
                fw.op("pe", mmu, reads=wb(Wu, fc * 128, (fc + 1) * 128) + [uT.b], writes=[pu.b])
                fw.op("act", lambda e, pg=pg, s=s: e.activation(out=s[:], in_=pg[:, 0:G], func=AF.Silu), reads=[pg.b], writes=[s.b])
                fw.op("dve", lambda e, pu=pu, s=s, fc=fc: e.tensor_tensor(out=aT[:, fc, :], in0=pu[:, 0:G], in1=s[:], op=ALU.mult),
                      reads=[pu.b, s.b], writes=[aT.b], add=(fc > 0))
            for j in range(TPG):
                t = g * TPG + j
                ht = hts[j]
                for half in range(2):
                    po = psO[half]

                    def mmd(e, po=po, j=j, half=half):
                        for fc in range(NFF):
                            i = e.matmul(po[:, 0:512], lhsT=aT[:, fc, j * 128:(j + 1) * 128], rhs=Wd[:, fc, half * 512:(half + 1) * 512],
                                         start=(fc == 0), stop=(fc == NFF - 1))
                        return i
                    fw.op("pe", mmd, reads=[aT.b] + wb(Wd, half * 512, (half + 1) * 512), writes=[po.b])
                    fw.op("dve", lambda e, po=po, ht=ht, half=half: e.tensor_tensor(out=ht[:, half * 512:(half + 1) * 512], in0=po[:, 0:512],
                                                                                   in1=ht[:, half * 512:(half + 1) * 512], op=ALU.add),
                          reads=[po.b, ht.b], writes=[ht.b])
                if not last:
                    if t == 0:
                        fw.op("pool", lambda e, ht=ht: e.memset(ht[0:FRONT, :], 0.0), reads=[ht.b], writes=[ht.b])
                    fw.dma("pool", C.Hd[t * 128:(t + 1) * 128, :], ht[:], reads=[ht.b])
                else:
                    if C.debug:
                        fw.dma("pool", C.Hd[t * 128:(t + 1) * 128, :], ht[:], reads=[ht.b])
                    if t == 0:
                        continue
                    ss = sss[t % 2]
                    fw.op("act", lambda e, ht=ht, ss=ss: e.activation(out=hn[:], in_=ht[:], func=AF.Square, accum_out=ss[:, 0:1]),
                          reads=[ht.b], writes=[hn.b, ss.b])
                    fw.op("act", lambda e, ss=ss: e.activation(out=ss[:, 1:2], in_=ss[:, 0:1], func=AF.Sqrt, bias=EPS, scale=1.0 / D_MODEL),
                          reads=[ss.b], writes=[ss.b])
                    fw.op("dve", lambda e, ss=ss: e.reciprocal(out=ss[:, 1:2], in_=ss[:, 1:2]), reads=[ss.b], writes=[ss.b])
                    fw.op("dve", lambda e, ht=ht, ss=ss: e.scalar_tensor_tensor(out=ht[:], in0=ht[:], scalar=ss[:, 1:2], in1=gF[:], op0=ALU.mult, op1=ALU.mult),
                          reads=[ht.b, ss.b, gF.b], writes=[ht.b])
                    fw.dma("pool", C.out[(t - 1) * 128:t * 128, :], ht[:], reads=[ht.b])
        fw.barrier()


_CONSTS = None
_FCOLS = None
ACTIVE = [0, 1, 4, 5]


def prep_inputs(inputs):
    global _CONSTS, _FCOLS
    if _CONSTS is None:
        _CONSTS = host_constants()
        _FCOLS = feature_cols()
    f32 = lambda a: np.ascontiguousarray(np.asarray(a, dtype=np.float32))
    w_in = f32(inputs["w_in"])
    sel = np.where(_FCOLS >= 0, _FCOLS, 0)
    w_mix = w_in[:, :, sel]
    w_mix[:, :, _FCOLS < 0] = 0.0
    w_mix = np.ascontiguousarray(w_mix)
    w_gz = np.ascontiguousarray(w_in[:, :, 2816:])
    shared = {
        "meta_tokens": f32(inputs["meta_tokens"]),
        "rel_bias": f32(inputs["rel_bias"]),
        "hgrn_lb_logits": f32(inputs["hgrn_lb_logits"]),
        "ln_mix": f32(inputs["ln_mix"]),
        "w_mix": w_mix,
        "w_gz": w_gz,
        "attn_sink": f32(inputs["attn_sink"]),
        "qk_norm_q": f32(inputs["qk_norm_q"]),
        "qk_norm_k": f32(inputs["qk_norm_k"]),
        "diff_lambda": f32(inputs["diff_lambda"]).reshape(2, 128),
        "diff_subnorm": f32(inputs["diff_subnorm"]),
        "hgrn_out_norm": f32(inputs["hgrn_out_norm"]),
        "w_branch": f32(inputs["w_branch"]),
        "w_out": f32(inputs["w_out"]),
        "ln_ffn": f32(inputs["ln_ffn"]),
        "w_ffn_gate": f32(inputs["w_ffn_gate"]),
        "w_ffn_up": f32(inputs["w_ffn_up"]),
        "w_ffn_down": f32(inputs["w_ffn_down"]),
        "ln_final": f32(inputs["ln_final"]).reshape(1, D_MODEL),
    }
    shared.update(_CONSTS)
    x = f32(inputs["x"])
    zx = np.zeros_like(x[0])
    zmeta = np.zeros_like(shared["meta_tokens"])
    in_maps = []
    for c in range(8):
        m = dict(shared)
        if c in ACTIVE:
            m["x"] = np.ascontiguousarray(x[ACTIVE.index(c)])
        else:
            m["x"] = zx
            m["meta_tokens"] = zmeta
        in_maps.append(m)
    return in_maps


def kernel(**inputs):
    import os
    in_maps = prep_inputs(inputs)
    ks = os.environ.get("KSTOP")
    if ks:
        a, b = ks.split(":")
        nc = build(stop_after=(a, int(b)))
    else:
        nc = build()
    res = run_bass_kernel_spmd(nc, in_maps, core_ids=list(range(8)))
    out = np.stack([res.results[c]["out"] for c in ACTIVE], axis=0)
    return out.astype(np.float32)
```
